# Optimizing a Trainium2 kernel written in Bass

```python
import math
import jax, jax.numpy as jnp
from jax import lax
import numpy as np

D_MODEL = 4096
BATCH = 1
SEQ = 8192
DEPTH = 1

CHUNK = 64
W_A = 4096
H_A = 16
BW_A = W_A // H_A
CONV_W = 4
LRU_C = 8.0
W_B = 4096
G_B = 16
DG_B = W_B // G_B
SPATIAL = 128
MIX = W_A + W_B
IN_COLS = 2 * W_A + 3 * W_B
EPS = 1e-6

kernel_name = "hybrid_rglru_gmlp_parallel_heads"


def rmsnorm(x, g):
    xf = x.astype(jnp.float32)
    y = xf * lax.rsqrt(jnp.mean(xf * xf, axis=-1, keepdims=True) + EPS)
    return (y * g.astype(jnp.float32)).astype(x.dtype)


def layernorm_f32(x, g, b):
    xf = x.astype(jnp.float32)
    mu = jnp.mean(xf, axis=-1, keepdims=True)
    var = jnp.mean(jnp.square(xf - mu), axis=-1, keepdims=True)
    return (xf - mu) * lax.rsqrt(var + EPS) * g.astype(jnp.float32) + b.astype(jnp.float32)


def causal_depthwise_conv(x, w, b):
    S = x.shape[1]
    xp = jnp.pad(x, ((0, 0), (CONV_W - 1, 0), (0, 0)))
    y = b
    for k in range(CONV_W):
        y = y + xp[:, k:k + S, :] * w[k]
    return y


def rg_lru(x, w_a, b_a, w_x, b_x, lam):
    B, S, _ = x.shape
    xh = x.reshape(B, S, H_A, BW_A)
    r = jax.nn.sigmoid(jnp.einsum('bshi,hij->bshj', xh, w_a) + b_a).reshape(B, S, W_A)
    i = jax.nn.sigmoid(jnp.einsum('bshi,hij->bshj', xh, w_x) + b_x).reshape(B, S, W_A)
    log_a = -LRU_C * r.astype(jnp.float32) * jax.nn.softplus(-lam.astype(jnp.float32))
    a = jnp.exp(log_a)
    inp = jnp.sqrt(-jnp.expm1(2.0 * log_a)) * (i.astype(jnp.float32) * x.astype(jnp.float32))

    def combine(left, right):
        a_l, b_l = left
        a_r, b_r = right
        return a_l * a_r, a_r * b_l + b_r

    _, h = lax.associative_scan(combine, (a, inp), axis=1)
    return h.astype(x.dtype)


def spatial_gating(u, v, ln_g, ln_b, w_sp, b_sp):
    B, S, _ = v.shape
    n = S // SPATIAL
    vn = layernorm_f32(v, ln_g, ln_b).reshape(B, n, SPATIAL, G_B, DG_B)
    blk = jnp.arange(SPATIAL) // CHUNK
    mask = (blk[None, :] <= blk[:, None]).astype(jnp.float32)
    ws = w_sp.astype(jnp.float32) * mask
    s = jnp.einsum('gij,bnjgd->bnigd', ws, vn) + b_sp.astype(jnp.float32).T[None, None, :, :, None]
    return (u.astype(jnp.float32) * s.reshape(B, S, W_B)).astype(u.dtype)


def setup_inputs(seed: int = 0) -> dict:
    key = jax.random.key(seed)
    ks = jax.random.split(key, 16)
    f32 = jnp.float32
    x = jax.random.normal(ks[0], (BATCH, SEQ, D_MODEL), f32)
    norm_g = 1.0 + 0.1 * jax.random.normal(ks[1], (DEPTH, D_MODEL), f32)
    w_in = jax.random.normal(ks[2], (DEPTH, D_MODEL, IN_COLS), f32) * D_MODEL ** -0.5
    conv_w = jax.random.normal(ks[3], (DEPTH, CONV_W, W_A), f32) * CONV_W ** -0.5
    conv_b = 0.01 * jax.random.normal(ks[4], (DEPTH, W_A), f32)
    w_gate_a = jax.random.normal(ks[5], (DEPTH, H_A, BW_A, BW_A), f32) * BW_A ** -0.5
    b_gate_a = 0.01 * jax.random.normal(ks[6], (DEPTH, H_A, BW_A), f32)
    w_gate_x = jax.random.normal(ks[7], (DEPTH, H_A, BW_A, BW_A), f32) * BW_A ** -0.5
    b_gate_x = 0.01 * jax.random.normal(ks[8], (DEPTH, H_A, BW_A), f32)
    u0 = jax.random.uniform(ks[9], (DEPTH, W_A), f32, minval=0.9, maxval=0.999)
    a0 = u0 ** (1.0 / LRU_C)
    lru_lambda = jnp.log(a0) - jnp.log1p(-a0)
    ln_v_g = 1.0 + 0.1 * jax.random.normal(ks[10], (DEPTH, W_B), f32)
    ln_v_b = 0.01 * jax.random.normal(ks[11], (DEPTH, W_B), f32)
    w_spatial = 0.5 * jax.random.normal(ks[12], (DEPTH, G_B, SPATIAL, SPATIAL), f32) * SPATIAL ** -0.5
    b_spatial = 1.0 + 0.1 * jax.random.normal(ks[13], (DEPTH, G_B, SPATIAL), f32)
    w_out = jax.random.normal(ks[14], (DEPTH, MIX, D_MODEL), f32) * MIX ** -0.5
    final_g = 1.0 + 0.1 * jax.random.normal(ks[15], (D_MODEL,), f32)
    return {"x": x, "norm_g": norm_g, "w_in": w_in, "conv_w": conv_w, "conv_b": conv_b,
            "w_gate_a": w_gate_a, "b_gate_a": b_gate_a, "w_gate_x": w_gate_x, "b_gate_x": b_gate_x,
            "lru_lambda": lru_lambda, "ln_v_g": ln_v_g, "ln_v_b": ln_v_b,
            "w_spatial": w_spatial, "b_spatial": b_spatial, "w_out": w_out, "final_g": final_g}


def reference(x, norm_g, w_in, conv_w, conv_b, w_gate_a, b_gate_a, w_gate_x, b_gate_x,
              lru_lambda, ln_v_g, ln_v_b, w_spatial, b_spatial, w_out, final_g):
    for l in range(DEPTH):
        hn = rmsnorm(x, norm_g[l])
        proj = jnp.einsum('bsd,de->bse', hn, w_in[l])
        xa, ga, u, v, gb = jnp.split(proj, [W_A, 2 * W_A, 2 * W_A + W_B, 2 * W_A + 2 * W_B], axis=-1)
        xa = causal_depthwise_conv(xa, conv_w[l], conv_b[l])
        ya = rg_lru(xa, w_gate_a[l], b_gate_a[l], w_gate_x[l], b_gate_x[l], lru_lambda[l])
        yb = spatial_gating(jax.nn.gelu(u, approximate=False), jax.nn.gelu(v, approximate=False),
                            ln_v_g[l], ln_v_b[l], w_spatial[l], b_spatial[l])
        mixed = jnp.concatenate([ya * jax.nn.silu(ga), yb * jax.nn.silu(gb)], axis=-1)
        x = x + jnp.einsum('bse,ed->bsd', mixed, w_out[l])
    return rmsnorm(x, final_g)
```

```python
import numpy as np
from contextlib import ExitStack
import concourse.bass as bass
import concourse.mybir as mybir
from concourse.bass_utils import run_bass_kernel_spmd

F32 = mybir.dt.float32
BF16 = mybir.dt.bfloat16
AF = mybir.ActivationFunctionType
ALU = mybir.AluOpType

NCORES = 8
D = 4096
TB = 1024
NBLK = 8
KT = D // 128
EPS = 1e-6


class Prog:
    ENG = ("pe", "act", "dve", "pool", "sp")

    def __init__(self, nc, stack):
        self.nc = nc
        self.stack = stack
        self.ops = {e: [] for e in self.ENG}
        self.esem = {e: stack.enter_context(nc.semaphore("S_" + e)) for e in self.ENG}
        self.ecount = {e: 0 for e in self.ENG}
        self.dsem = {}
        self.lastw = {}
        self.readers = {}
        self.waited = {e: {} for e in self.ENG}

    def _need(self, eng, toks):
        best = {}
        for t in toks:
            if t is None:
                continue
            sem, val, src = t
            if src == eng and eng == "pe":
                continue
            k = id(sem)
            if k not in best or best[k][1] < val:
                best[k] = (sem, val)
        out = []
        w = self.waited[eng]
        for k, (sem, val) in best.items():
            if k in w and w[k] >= val:
                continue
            w[k] = val
            out.append((sem, val))
        return out

    def _deps(self, reads, writes):
        toks = []
        for k in reads:
            toks.append(self.lastw.get(k))
        for k in writes:
            toks.append(self.lastw.get(k))
            toks.extend(self.readers.get(k, []))
        return toks

    def _commit(self, tok, reads, writes):
        for k in reads:
            self.readers.setdefault(k, []).append(tok)
        for k in writes:
            self.lastw[k] = tok
            self.readers[k] = []

    def op(self, eng, fn, reads=(), writes=()):
        waits = self._need(eng, self._deps(reads, writes))
        self.ecount[eng] += 1
        tok = (self.esem[eng], self.ecount[eng], eng)
        self.ops[eng].append((waits, fn, (self.esem[eng], 1)))
        self._commit(tok, reads, writes)
        return tok

    def dma(self, eng, fn, reads=(), writes=(), semkey=None):
        if semkey is None:
            semkey = ("dma",) + tuple(writes if writes else reads)
        if semkey not in self.dsem:
            self.dsem[semkey] = [self.stack.enter_context(
                self.nc.semaphore("D%d" % len(self.dsem))), 0]
        ent = self.dsem[semkey]
        prev = (ent[0], ent[1], "dma") if ent[1] > 0 else None
        waits = self._need(eng, self._deps(reads, writes) + [prev])
        ent[1] += 16
        tok = (ent[0], ent[1], "dma")
        self.ops[eng].append((waits, fn, (ent[0], 16)))
        self._commit(tok, reads, writes)
        return tok

    def finish(self, eng, toks):
        waits = self._need(eng, toks)
        self.ops[eng].append((waits, None, None))

    def emit(self, block):
        names = {"pe": "tensor", "act": "scalar", "dve": "vector", "pool": "gpsimd", "sp": "sync"}
        for e in self.ENG:
            lst = self.ops[e]

            def body(engine, lst=lst):
                for waits, fn, inc in lst:
                    for sem, val in waits:
                        engine.wait_ge(sem, val)
                    if fn is None:
                        continue
                    ins = fn(engine)
                    if inc is not None:
                        ins.then_inc(inc[0], inc[1])
            getattr(block, names[e])(body)


def build_nc():
    nc = bass.Bass("TRN2", target_bir_lowering=False)

    def din(n, s):
        return nc.dram_tensor(n, s, F32, kind="ExternalInput").ap()

    xs_d = din("xs", [NBLK, TB, D])
    cmask_d = din("cmask", [128, NBLK])
    w_in_d = din("w_in", [D, 20480])
    w_out_d = din("w_out", [8192, D])
    fgbc_d = din("fgbc", [128, D])
    ngbc_d = din("ngbc", [128, D])
    cw_d = din("cw", [128, 32 * 4])
    par_d = din("par", [128, 7 * 32])
    wga_d = din("wga", [16, 256, 256])
    wgx_d = din("wgx", [16, 256, 256])
    wsT_d = din("wsT", [128, 16 * 128])
    bsp_d = din("bspbc", [128, 16 * 128])
    out_d = nc.dram_tensor("out", [TB, D], F32, kind="ExternalOutput").ap()
    mixed_d = nc.dram_tensor("mixed_d", [64, 128, TB], BF16, kind="Internal").ap()
    gv_d = nc.dram_tensor("gv_d", [16, 128, 2048], F32, kind="Internal").ap()
    y_d = nc.dram_tensor("y_d", [TB, D], F32, kind="Internal").ap()

    w_in_v = w_in_d.rearrange("(kt p) c -> p kt c", p=128)
    w_out_v = w_out_d.rearrange("(kk p) c -> p kk c", p=128)

    st = ExitStack()
    with st:
        def sb(n, s, d=F32):
            return st.enter_context(nc.sbuf_tensor(n, s, d))
        big = sb("big", [128, 32768], BF16)
        wbuf = sb("wbuf", [128, 3, KT, 256], BF16)
        SC = sb("SC", [128, 14, 1024], F32)
        xar = sb("xar", [128, 1027], F32)
        ybf = sb("ybf", [128, 4096], BF16)
        wg = sb("wg", [128, 2, 2, 2, 256], BF16)
        mxo = sb("mxo", [128, 1024], BF16)
        gbx = sb("gbx", [128, D], F32)
        rsbc = gbx[:, 0:2048].rearrange("p (a b) -> p a b", a=16)
        wsTb = gbx[:, 2048:3072].bitcast(BF16).rearrange("p (a b) -> p a b", a=16)
        t2 = gbx[:, 3072:3328].rearrange("p (a b) -> p a b", a=2)
        bspg = gbx[:, 3328:3456]
        cw = sb("cw_s", [128, 32, 4], F32)
        par = sb("par_s", [128, 7, 32], F32)
        c8 = sb("c8", [128, 32], F32)
        c16 = sb("c16", [128, 32], F32)
        cmask = sb("cmask_s", [128, NBLK], F32)
        hstate = sb("hstate", [128, 32], F32)
        halo = sb("halo", [128, 32, 3], F32)
        initt = sb("initt", [128, 2], F32)
        identb = sb("identb", [128, 128], BF16)
        ssq0 = sb("ssq0", [128, 64], F32)
        rstd0 = sb("rstd0", [128, 64], F32)
        s1p = sb("s1p", [128, 8, 16], F32)
        s2p = sb("s2p", [128, 8, 16], F32)
        lnst = sb("lnst", [128, 5, 8], F32)
        ssqp = sb("ssqp", [128, 8, 16], F32)
        fst = sb("fst", [128, 2, 8], F32)
        psA = st.enter_context(nc.psum_tensor("psA", [128, 2048], F32))
        psB = st.enter_context(nc.psum_tensor("psB", [128, 2048], F32))

        hnT = big[:].rearrange("p (k t) -> p k t", k=KT)
        mixh = big[:].rearrange("p (k t) -> p k t", k=64)
        ybf3 = ybf[:].rearrange("p (q c t) -> p q c t", q=2, c=2)
        zb = ybf[:, 0:2048].rearrange("p (t c) -> p t c", t=8)
        CB, BGA, BGX, LAM, LNG, LNB, NG = range(7)

        P = Prog(nc, st)
        A_ = lambda k: "psA%d" % k
        B_ = lambda k: "psB%d" % k
        S_ = lambda k: "S%d" % k

        P.dma("sp", lambda e: e.dma_start(out=cw[:].rearrange("p a b -> p (a b)"), in_=cw_d[:, :]), writes=["cw"])
        P.dma("sp", lambda e: e.dma_start(out=par[:].rearrange("p a b -> p (a b)"), in_=par_d[:, :]), writes=["par"])
        P.dma("sp", lambda e: e.dma_start(out=cmask[:], in_=cmask_d[:, :]), writes=["cmask"])
        P.dma("sp", lambda e: e.dma_start(out=gbx[:], in_=ngbc_d[:, :]), writes=["gbx"])
        identf = SC[:, 13, 0:128]
        P.op("pool", lambda e: e.memset(identf, 0.0), writes=[S_(13)])
        P.op("pool", lambda e: e.affine_select(out=identf, in_=identf, compare_op=ALU.not_equal,
                                               fill=1.0, base=0, pattern=[[-1, 128]], channel_multiplier=1),
             reads=[S_(13)], writes=[S_(13)])
        P.op("pool", lambda e: e.tensor_copy(out=identb[:], in_=identf), reads=[S_(13)], writes=["ident"])
        P.op("dve", lambda e: e.memset(hstate[:], 0.0), writes=["hstate"])
        P.op("dve", lambda e: e.memset(halo[:].rearrange("p a b -> p (a b)"), 0.0), writes=["halo"])
        for ap_, nm in [(ssq0[:], "ssq0"), (s1p[:].rearrange("p a b -> p (a b)"), "s1p"),
                        (s2p[:].rearrange("p a b -> p (a b)"), "s2p"), (ssqp[:].rearrange("p a b -> p (a b)"), "ssqp")]:
            P.op("dve", lambda e, ap_=ap_: e.memset(ap_, 0.0), writes=[nm])
        P.op("act", lambda e: e.activation(out=c8[:], in_=par[:, LAM, :], func=AF.Exp, scale=-1.0), reads=["par"], writes=["c8"])
        P.op("act", lambda e: e.activation(out=c8[:], in_=c8[:], func=AF.Ln, bias=1.0, scale=1.0), reads=["c8"], writes=["c8"])
        P.op("dve", lambda e: e.tensor_scalar(out=c16[:], in0=c8[:], scalar1=-16.0, scalar2=None, op0=ALU.mult), reads=["c8"], writes=["c16"])
        P.op("dve", lambda e: e.tensor_scalar(out=c8[:], in0=c8[:], scalar1=-8.0, scalar2=None, op0=ALU.mult), reads=["c8", "c16"], writes=["c8"])
        wstate = {"n": 0}

        def wload(src_ap):
            slot = wstate["n"] % 3
            wstate["n"] += 1
            if isinstance(src_ap, tuple):
                dst = wbuf[:, slot].rearrange("p k c -> p (k c)").bitcast(F32)
                P.dma("pool", lambda e, dst=dst, ap=src_ap[1]: e.dma_start(out=dst, in_=ap), writes=["w%d" % slot])
            else:
                P.dma("pool", lambda e, slot=slot, src_ap=src_ap: e.dma_start(out=wbuf[:, slot], in_=src_ap),
                      writes=["w%d" % slot])
            return slot

        def wcols(c0):
            return w_in_v[:, :, c0:c0 + 256]

        class WQ:
            def __init__(self, srcs):
                self.srcs = srcs
                self.slots = {}
                self.next = 0

            def prefetch(self, upto):
                while self.next <= min(upto, len(self.srcs) - 1):
                    self.slots[self.next] = wload(self.srcs[self.next])
                    self.next += 1

            def get(self, i, ahead=2):
                self.prefetch(i + ahead)
                return self.slots[i]

        srcs = []
        idx = {}
        for j in range(NBLK - 1):
            if j >= 1:
                idx[("x0", j)] = len(srcs); srcs.append(("x", xs_d[j, 0:128, :]))
            for h in range(16):
                idx[("xa", j, h)] = len(srcs); srcs.append(wcols(h * 256))
        idx[("x0", NBLK - 1)] = len(srcs); srcs.append(("x", xs_d[NBLK - 1, 0:128, :]))
        idx[("xa", NBLK - 1, 0)] = len(srcs); srcs.append(wcols(0))
        for h in range(16):
            if h + 1 < 16:
                idx[("xa", NBLK - 1, h + 1)] = len(srcs); srcs.append(wcols((h + 1) * 256))
            idx[("ga", h)] = len(srcs); srcs.append(wcols(4096 + h * 256))
        for g in range(16):
            idx[("v", g)] = len(srcs); srcs.append(wcols(3 * 4096 + g * 256))
        for g in range(16):
            idx[("u", g)] = len(srcs); srcs.append(wcols(2 * 4096 + g * 256))
            idx[("gb", g)] = len(srcs); srcs.append(wcols(4 * 4096 + g * 256))
        for cbk in range(16):
            idx[("o", cbk, 0)] = len(srcs); srcs.append(w_out_v[:, 0:32, cbk * 256:(cbk + 1) * 256])
            idx[("o", cbk, 1)] = len(srcs); srcs.append(w_out_v[:, 32:64, cbk * 256:(cbk + 1) * 256])
        wq = WQ(srcs)

        def proj_cm(slot, bank0=0):
            for ct in range(2):
                for th in range(2):
                    bk = ct * 2 + th

                    def fn(e, ct=ct, th=th, bk=bk):
                        ins = None
                        for kt in range(KT):
                            ins = e.matmul(psA[:, bk * 512:(bk + 1) * 512], lhsT=wbuf[:, slot, kt, ct * 128:(ct + 1) * 128],
                                           rhs=hnT[:, kt, th * 512:(th + 1) * 512], start=(kt == 0), stop=(kt == KT - 1))
                        return ins
                    P.op("pe", fn, reads=["w%d" % slot, ("hnT", "d"), ("hnT", "a")], writes=[A_(bk)])

        for j in range(NBLK):
            own = (j == NBLK - 1)
            YB4 = [("ybf", 0, 0), ("ybf", 0, 1), ("ybf", 1, 0), ("ybf", 1, 1)]
            xn = ybf[:, :]
            junk = SC[:, 12:14, :].rearrange("p a b -> p (a b)").bitcast(BF16)

            x0slot = wq.get(idx[("x0", j)]) if ("x0", j) in idx else None

            def p0_xt(tt, j=j, x0slot=x0slot):
                if tt == 0 and x0slot is not None:
                    return None, ["w%d" % x0slot], wbuf[:, x0slot].rearrange("p k c -> p (k c)").bitcast(F32)
                xb = (j * 8 + tt) % 3
                return xb, [S_(4 * xb + i) for i in range(4)], SC[:, 4 * xb:4 * xb + 4, :].rearrange("p a b -> p (a b)")

            def p0_fa(tt, j=j):
                xb, xkeys, xt = p0_xt(tt)
                col = j * 8 + tt
                if xb is not None:
                    P.dma("sp", lambda e, xt=xt, j=j, tt=tt: e.dma_start(out=xt, in_=xs_d[j, tt * 128:(tt + 1) * 128, :]),
                          writes=xkeys, semkey=("xt", xb))
                P.op("act", lambda e, xt=xt, col=col: e.activation(out=junk, in_=xt, func=AF.Square, accum_out=ssq0[:, col:col + 1]),
                     reads=xkeys + ["ssq0"], writes=[S_(12), S_(13), ("ssq0", col)])
                P.op("act", lambda e, col=col: e.activation(out=rstd0[:, col:col + 1], in_=ssq0[:, col:col + 1], func=AF.Sqrt,
                                                            scale=1.0 / D, bias=EPS),
                     reads=[("ssq0", col)], writes=[("rstd0", col)])
                P.op("dve", lambda e, col=col: e.reciprocal(out=rstd0[:, col:col + 1], in_=rstd0[:, col:col + 1]),
                     reads=[("rstd0", col)], writes=[("rstd0", col)])

            def p0_stt(tt, j=j):
                xb, xkeys, xt = p0_xt(tt)
                col = j * 8 + tt
                P.op("dve", lambda e, xt=xt, col=col: e.scalar_tensor_tensor(out=xn, in0=xt, scalar=rstd0[:, col:col + 1], in1=gbx[:],
                                                                            op0=ALU.mult, op1=ALU.mult),
                     reads=xkeys + [("rstd0", col), "gbx"], writes=YB4)

            def p0_back(tt, j=j):
                ps = psA if tt % 2 == 0 else psB
                pk = A_ if tt % 2 == 0 else B_
                ps16 = ps[:, :].bitcast(BF16)
                for q in range(4):
                    def fn(e, q=q, ps16=ps16):
                        ins = None
                        for i in range(8):
                            kt = q * 8 + i
                            ins = e.transpose(out=ps16[:, q * 1024 + i * 128: q * 1024 + (i + 1) * 128],
                                              in_=xn[:, kt * 128:(kt + 1) * 128], identity=identb[:])
                        return ins
                    P.op("pe", fn, reads=YB4 + ["ident"], writes=[pk(q)])
                    src = ps16[:, q * 1024:(q + 1) * 1024].rearrange("p (a b) -> p a b", a=8)
                    dst = hnT[:, q * 8:(q + 1) * 8, tt * 128:(tt + 1) * 128]
                    if q == 0:
                        P.op("dve", lambda e, src=src, dst=dst: e.tensor_copy(out=dst, in_=src), reads=[pk(q)], writes=[("hnT", "d")])
                    else:
                        P.op("act", lambda e, src=src, dst=dst: e.copy(out=dst, in_=src), reads=[pk(q)], writes=[("hnT", "a")])

            p0_fa(0)
            p0_stt(0)
            for tt in range(8):
                if tt + 1 < 8:
                    p0_fa(tt + 1)
                p0_back(tt)
                if tt + 1 < 8:
                    p0_stt(tt + 1)

            def stage1(h, p, j=j):
                slot = wq.get(idx[("xa", j, h)])
                P.dma("pool", lambda e, p=p, h=h: e.dma_start(out=wg[:, p, 0], in_=wga_d[h].rearrange("(it p) j -> p it j", p=128)),
                      writes=[("wg", p, 0)])
                P.dma("pool", lambda e, p=p, h=h: e.dma_start(out=wg[:, p, 1], in_=wgx_d[h].rearrange("(it p) j -> p it j", p=128)),
                      writes=[("wg", p, 1)])
                proj_cm(slot)
                stage1_ct(h, p, 0)

            def stage1_ct(h, p, ct, j=j):
                if True:
                    ctg = h * 2 + ct
                    ysl = ct if p == 0 else 12 + ct
                    yk = S_(ysl)
                    y = SC[:, ysl, :]
                    P.op("dve", lambda e, ct=ct, ctg=ctg: e.tensor_copy(out=xar[:, 0:3], in_=halo[:, ctg, :]),
                         reads=["halo"], writes=["xarh"])
                    for th in range(2):
                        bk = ct * 2 + th
                        P.op("act", lambda e, ct=ct, th=th, bk=bk: e.copy(out=xar[:, 3 + th * 512: 3 + (th + 1) * 512],
                                                                          in_=psA[:, bk * 512:(bk + 1) * 512]),
                             reads=[A_(bk)], writes=[("xar", th)])
                    xk = ["xarh", ("xar", 0), ("xar", 1)]
                    P.op("dve", lambda e, ct=ct, ctg=ctg, y=y: e.tensor_scalar(out=y, in0=xar[:, 3:1027], scalar1=cw[:, ctg, 3:4],
                                                                             scalar2=par[:, CB, ctg:ctg + 1], op0=ALU.mult, op1=ALU.add),
                         reads=xk + ["cw", "par"], writes=[yk])
                    for k in range(3):
                        P.op("dve", lambda e, ct=ct, ctg=ctg, y=y, k=k: e.scalar_tensor_tensor(
                            out=y, in0=xar[:, k:k + 1024], scalar=cw[:, ctg, k:k + 1], in1=y, op0=ALU.mult, op1=ALU.add),
                            reads=xk + ["cw", yk], writes=[yk])
                    P.op("dve", lambda e, ct=ct, ctg=ctg: e.tensor_copy(out=halo[:, ctg, :], in_=xar[:, 1024:1027]),
                         reads=xk, writes=["halo"])
                    P.op("dve", lambda e, ct=ct, y=y, p=p: e.tensor_copy(out=ybf3[:, p, ct, :], in_=y), reads=[yk], writes=[("ybf", p, ct)])

            def stage2(h, p, j=j, own=own):
                for jt in range(2):
                    ctg = h * 2 + jt
                    base = 2 + 5 * jt
                    for gate in range(2):
                        for th in range(2):
                            bk = gate * 2 + th

                            def fn(e, gate=gate, jt=jt, th=th, bk=bk, p=p):
                                ins = None
                                for it in range(2):
                                    ins = e.matmul(psB[:, bk * 512:(bk + 1) * 512], lhsT=wg[:, p, gate, it, jt * 128:(jt + 1) * 128],
                                                   rhs=ybf3[:, p, it, th * 512:(th + 1) * 512], start=(it == 0), stop=(it == 1))
                                return ins
                            P.op("pe", fn, reads=[("wg", p, gate), ("ybf", p, 0), ("ybf", p, 1)], writes=[B_(bk)])
                    for gate in range(2):
                        dst = SC[:, base + (0 if gate == 0 else 2), :]
                        dk = S_(base + (0 if gate == 0 else 2))
                        bcol = BGA if gate == 0 else BGX
                        for th in range(2):
                            bk = gate * 2 + th
                            P.op("act", lambda e, dst=dst, th=th, bk=bk, bcol=bcol, ctg=ctg: e.activation(
                                out=dst[:, th * 512:(th + 1) * 512], in_=psB[:, bk * 512:(bk + 1) * 512], func=AF.Sigmoid,
                                bias=par[:, bcol, ctg:ctg + 1], scale=1.0), reads=[B_(bk), "par"], writes=[dk])

            def stage2_rest(h, p, j=j, own=own):
                for jt in range(2):
                    ctg = h * 2 + jt
                    base = 2 + 5 * jt
                    r_, a2_ = SC[:, base, :], SC[:, base + 1, :]
                    rk, a2k = S_(base), S_(base + 1)
                    P.op("act", lambda e, r_=r_, a2_=a2_, ctg=ctg: e.activation(out=a2_, in_=r_, func=AF.Exp, scale=c16[:, ctg:ctg + 1]),
                         reads=[rk, "c16"], writes=[a2k])
                    P.op("act", lambda e, r_=r_, ctg=ctg: e.activation(out=r_, in_=r_, func=AF.Exp, scale=c8[:, ctg:ctg + 1]),
                         reads=[rk, "c8"], writes=[rk])
                for jt in range(2):
                    base = 2 + 5 * jt
                    a2_, a2k = SC[:, base + 1, :], S_(base + 1)
                    P.op("dve", lambda e, a2_=a2_: e.tensor_scalar(out=a2_, in0=a2_, scalar1=1.0, scalar2=None, op0=ALU.min),
                         reads=[a2k], writes=[a2k])
                    P.op("act", lambda e, a2_=a2_: e.activation(out=a2_, in_=a2_, func=AF.Sqrt, scale=-1.0, bias=1.0),
                         reads=[a2k], writes=[a2k])
                for jt in range(2):
                    ctg = h * 2 + jt
                    base = 2 + 5 * jt
                    ysl = jt if p == 0 else 12 + jt
                    r_, a2_, i_, b_, ho_ = [SC[:, base + q, :] for q in range(5)]
                    rk, a2k, ik, bk_, hok = [S_(base + q) for q in range(5)]
                    P.op("dve", lambda e, a2_=a2_, i_=i_, b_=b_: e.tensor_tensor(out=b_, in0=a2_, in1=i_, op=ALU.mult),
                         reads=[a2k, ik], writes=[bk_])
                    P.op("dve", lambda e, b_=b_, ysl=ysl: e.tensor_tensor(out=b_, in0=b_, in1=SC[:, ysl, :], op=ALU.mult),
                         reads=[bk_, S_(ysl)], writes=[bk_])
                    P.op("dve", lambda e, ctg=ctg, jt=jt, j=j: e.tensor_tensor(out=initt[:, jt:jt + 1], in0=hstate[:, ctg:ctg + 1],
                                                                             in1=cmask[:, j:j + 1], op=ALU.mult),
                         reads=["hstate", "cmask"], writes=[("initt", jt)])
                    P.op("dve", lambda e, r_=r_, b_=b_, ho_=ho_, jt=jt: e.tensor_tensor_scan(
                        out=ho_, data0=r_, data1=b_, initial=initt[:, jt:jt + 1], op0=ALU.mult, op1=ALU.add),
                        reads=[rk, bk_, ("initt", jt)], writes=[hok])
                    P.op("dve", lambda e, ho_=ho_, ctg=ctg: e.tensor_copy(out=hstate[:, ctg:ctg + 1], in_=ho_[:, 1023:1024]),
                         reads=[hok], writes=["hstate"])
                if own:
                    slot2 = wq.get(idx[("ga", h)])
                    proj_cm(slot2)
                    for ct in range(2):
                        ctg = h * 2 + ct
                        base = 2 + 5 * ct
                        sg = SC[:, base, :]
                        sgk = [S_(base)]
                        for th in range(2):
                            bk = ct * 2 + th
                            P.op("act", lambda e, sg=sg, th=th, bk=bk: e.activation(out=sg[:, th * 512:(th + 1) * 512],
                                                                                   in_=psA[:, bk * 512:(bk + 1) * 512], func=AF.Silu),
                                 reads=[A_(bk)], writes=sgk)
                        P.op("dve", lambda e, sg=sg, base=base: e.tensor_tensor(out=mxo[:], in0=SC[:, base + 4, :], in1=sg, op=ALU.mult),
                             reads=sgk + [S_(base + 4)], writes=["mxo"])
                        P.dma("sp", lambda e, ctg=ctg: e.dma_start(out=mixed_d[ctg, :, :], in_=mxo[:]),
                              reads=["mxo"], writes=[("mixed_d", ctg)], semkey="mxo_st")

            stage1(0, 0)
            stage1_ct(0, 0, 1)
            for h in range(16):
                if h + 1 < 16:
                    stage1(h + 1, (h + 1) % 2)
                stage2(h, h % 2)
                if h + 1 < 16:
                    stage1_ct(h + 1, (h + 1) % 2, 1)
                stage2_rest(h, h % 2)


        P.op("dve", lambda e: e.memset(t2[:].rearrange("p a b -> p (a b)"), 0.0),
             writes=["gbx", "rsbc", "wsTb", "bspg", ("t2", 0), ("t2", 1)])
        wsTf = SC[:, 0:2, :].rearrange("p a b -> p (a b)")
        wsTf3 = wsTf.rearrange("p (g i) -> p g i", g=16)
        onesv = SC[:, 2, 0:128]
        P.dma("sp", lambda e: e.dma_start(out=wsTf, in_=wsT_d[:, :]), writes=[S_(0), S_(1)])
        P.op("dve", lambda e: e.memset(onesv, 1.0), writes=[S_(2)])
        P.op("dve", lambda e: e.memset(wsTf3[64:128, :, 0:64], 0.0), reads=[], writes=[S_(0), S_(1)])
        P.op("dve", lambda e: e.tensor_copy(out=wsTb[:].rearrange("p a b -> p (a b)"), in_=wsTf), reads=[S_(0), S_(1)], writes=["wsTb"])
        for q in range(4):
            P.op("pe", lambda e, q=q: e.matmul(psA[:, q * 512:(q + 1) * 512], lhsT=onesv, rhs=wsTf[:, q * 512:(q + 1) * 512],
                                               start=True, stop=True), reads=[S_(2), S_(0), S_(1)], writes=[A_(q)])
            P.op("act", lambda e, q=q: e.copy(out=rsbc[:].rearrange("p a b -> p (a b)")[:, q * 512:(q + 1) * 512],
                                              in_=psA[:, q * 512:(q + 1) * 512]), reads=[A_(q)], writes=["rsbc"])

        for g in range(16):
            slot = wq.get(idx[("v", g)])
            sb0 = 0 if g % 2 == 0 else 3
            stg = SC[:, sb0:sb0 + 2, :].rearrange("p a b -> p (a b)").rearrange("p (t c) -> p t c", t=8)
            for tt in range(8):
                bk = tt % 4

                def fn(e, tt=tt, bk=bk, slot=slot):
                    ins = None
                    for kt in range(KT):
                        ins = e.matmul(psA[:, bk * 512: bk * 512 + 256], lhsT=hnT[:, kt, tt * 128:(tt + 1) * 128],
                                       rhs=wbuf[:, slot, kt, :], start=(kt == 0), stop=(kt == KT - 1))
                    return ins
                P.op("pe", fn, reads=["w%d" % slot, ("hnT", "d"), ("hnT", "a")], writes=[A_(bk)])
                P.op("act", lambda e, tt=tt, bk=bk, g=g, stg=stg: e.activation(out=stg[:, tt, :], in_=psA[:, bk * 512: bk * 512 + 256],
                                                                              func=AF.Gelu, accum_out=s1p[:, tt, g:g + 1]),
                     reads=[A_(bk), "s1p"], writes=[S_(sb0 + tt // 4), ("s1p", tt, g)])
                P.op("act", lambda e, tt=tt, g=g, stg=stg: e.activation(out=SC[:, 2, 0:256], in_=stg[:, tt, :], func=AF.Square,
                                                                       accum_out=s2p[:, tt, g:g + 1]),
                     reads=[S_(sb0 + tt // 4), "s2p"], writes=[S_(2), ("s2p", tt, g)])
            P.dma("sp", lambda e, g=g, sb0=sb0: e.dma_start(out=gv_d[g, :, :], in_=SC[:, sb0:sb0 + 2, :].rearrange("p a b -> p (a b)")),
                  reads=[S_(sb0), S_(sb0 + 1)], writes=[("gv_d", g)],
                  semkey=("gv_st", g % 2))
        allp = [("s1p", t_, g_) for t_ in range(8) for g_ in range(16)] + [("s2p", t_, g_) for t_ in range(8) for g_ in range(16)]
        P.op("dve", lambda e: e.reduce_sum(out=lnst[:, 0, :], in_=s1p[:], axis=mybir.AxisListType.X), reads=allp + ["s1p"], writes=["ln0"])
        P.op("dve", lambda e: e.reduce_sum(out=lnst[:, 1, :], in_=s2p[:], axis=mybir.AxisListType.X), reads=allp + ["s2p"], writes=["ln1"])
        P.op("dve", lambda e: e.tensor_scalar(out=lnst[:, 0, :], in0=lnst[:, 0, :], scalar1=1.0 / D, scalar2=None, op0=ALU.mult),
             reads=["ln0"], writes=["ln0"])
        P.op("dve", lambda e: e.tensor_tensor(out=lnst[:, 2, :], in0=lnst[:, 0, :], in1=lnst[:, 0, :], op=ALU.mult),
             reads=["ln0"], writes=["ln2"])
        P.op("dve", lambda e: e.scalar_tensor_tensor(out=lnst[:, 1, :], in0=lnst[:, 1, :], scalar=1.0 / D, in1=lnst[:, 2, :],
                                                     op0=ALU.mult, op1=ALU.subtract), reads=["ln1", "ln2"], writes=["ln1"])
        P.op("act", lambda e: e.activation(out=lnst[:, 3, :], in_=lnst[:, 1, :], func=AF.Sqrt, scale=1.0, bias=EPS),
             reads=["ln1"], writes=["ln3"])
        P.op("dve", lambda e: e.reciprocal(out=lnst[:, 3, :], in_=lnst[:, 3, :]), reads=["ln3"], writes=["ln3"])
        P.op("dve", lambda e: e.scalar_tensor_tensor(out=lnst[:, 4, :], in0=lnst[:, 0, :], scalar=-1.0, in1=lnst[:, 3, :],
                                                     op0=ALU.mult, op1=ALU.mult), reads=["ln0", "ln3"], writes=["ln4"])

        for g in range(16):
            gvg = SC[:, 10:12, :].rearrange("p a b -> p (a b)").rearrange("p (t c) -> p t c", t=8)
            P.dma("sp", lambda e, g=g: e.dma_start(out=SC[:, 10:12, :].rearrange("p a b -> p (a b)"), in_=gv_d[g, :, :]),
                  reads=[("gv_d", g)], writes=[S_(10), S_(11)], semkey="gv_ld")
            for tt in range(8):
                P.op("dve", lambda e, tt=tt, gvg=gvg: e.tensor_scalar(out=zb[:, tt, :], in0=gvg[:, tt, :], scalar1=lnst[:, 3, tt:tt + 1],
                                                                    scalar2=lnst[:, 4, tt:tt + 1], op0=ALU.mult, op1=ALU.add),
                     reads=[S_(10), S_(11), "ln3", "ln4"], writes=[("ybf", 0, 0), ("ybf", 0, 1)])
            P.dma("sp", lambda e, g=g: e.dma_start(out=bspg[:], in_=bsp_d[:, g * 128:(g + 1) * 128]), writes=["bspg"])
            slot_u = wq.get(idx[("u", g)])
            proj_cm(slot_u)
            for ct in range(2):
                for th in range(2):
                    bk = ct * 2 + th
                    P.op("act", lambda e, ct=ct, th=th, bk=bk: e.activation(out=SC[:, ct, th * 512:(th + 1) * 512],
                                                                          in_=psA[:, bk * 512:(bk + 1) * 512], func=AF.Gelu),
                         reads=[A_(bk)], writes=[S_(ct)])
            slot_g = wq.get(idx[("gb", g)])
            proj_cm(slot_g)
            for ct in range(2):
                for th in range(2):
                    bk = ct * 2 + th
                    P.op("act", lambda e, ct=ct, th=th, bk=bk: e.activation(out=SC[:, 2 + ct, th * 512:(th + 1) * 512],
                                                                          in_=psA[:, bk * 512:(bk + 1) * 512], func=AF.Silu),
                         reads=[A_(bk)], writes=[S_(2 + ct)])
            for ct in range(2):
                ctg = g * 2 + ct

                def fn(e, ct=ct, g=g):
                    ins = None
                    for tt in range(8):
                        ins = e.matmul(psB[:, ct * 1024 + tt * 128: ct * 1024 + (tt + 1) * 128], lhsT=zb[:, tt, ct * 128:(ct + 1) * 128],
                                       rhs=wsTb[:, g, :], start=True, stop=True)
                    return ins
                P.op("pe", fn, reads=[("ybf", 0, 0), ("ybf", 0, 1), "wsTb"], writes=[B_(2 * ct), B_(2 * ct + 1)])
                P.op("dve", lambda e, ct=ct, ctg=ctg, g=g: e.scalar_tensor_tensor(out=t2[:, ct, :], in0=rsbc[:, g, :],
                                                                                scalar=par[:, LNB, ctg:ctg + 1], in1=bspg[:],
                                                                                op0=ALU.mult, op1=ALU.add),
                     reads=["rsbc", "par", "bspg"], writes=[("t2", ct)])
                sk = S_(4 + ct)
                s_ = SC[:, 4 + ct, :]
                for tt in range(8):
                    P.op("dve", lambda e, ct=ct, ctg=ctg, tt=tt, s_=s_: e.scalar_tensor_tensor(
                        out=s_[:, tt * 128:(tt + 1) * 128], in0=psB[:, ct * 1024 + tt * 128: ct * 1024 + (tt + 1) * 128],
                        scalar=par[:, LNG, ctg:ctg + 1], in1=t2[:, ct, :], op0=ALU.mult, op1=ALU.add),
                        reads=[B_(2 * ct + tt // 4), ("t2", ct), "par"], writes=[sk])
                sks = [sk]
                P.op("dve", lambda e, ct=ct, s_=s_: e.tensor_tensor(out=s_, in0=s_, in1=SC[:, ct, :], op=ALU.mult),
                     reads=sks + [S_(ct)], writes=sks)
                P.op("dve", lambda e, ct=ct, s_=s_: e.tensor_tensor(out=mxo[:], in0=s_, in1=SC[:, 2 + ct, :], op=ALU.mult),
                     reads=sks + [S_(2 + ct)], writes=["mxo"])
                P.dma("sp", lambda e, ctg=ctg: e.dma_start(out=mixed_d[32 + ctg, :, :], in_=mxo[:]),
                      reads=["mxo"], writes=[("mixed_d", 32 + ctg)], semkey="mxo_st")

        mixed_v = mixed_d.rearrange("k p t -> p k t")
        scv = SC[:].rearrange("p a b -> p (a b)").bitcast(BF16).rearrange("p (k t) -> p k t", k=28)
        ybv = ybf[:].rearrange("p (k t) -> p k t", k=4)
        SCK = [S_(k_) for k_ in range(14)]
        YBK = [("ybf", 0, 0), ("ybf", 0, 1), ("ybf", 1, 0), ("ybf", 1, 1)]
        HK = [("hnT", "d"), ("hnT", "a")]
        MIXK = HK + SCK + YBK

        def mix(kk):
            if kk < 32:
                return hnT[:, kk, :]
            if kk < 60:
                return scv[:, kk - 32, :]
            return ybv[:, kk - 60, :]
        for (k0, k1, dst, wk) in [(0, 16, hnT[:, 0:16, :], HK), (16, 32, hnT[:, 16:32, :], HK), (32, 48, scv[:, 0:16, :], SCK),
                                  (48, 60, scv[:, 16:28, :], SCK), (60, 64, ybv[:, :, :], YBK)]:
            P.dma("pool" if k0 < 32 else "sp", lambda e, k0=k0, k1=k1, dst=dst: e.dma_start(out=dst, in_=mixed_v[:, k0:k1, :]),
                  reads=[("mixed_d", k_) for k_ in range(k0, k1)], writes=wk, semkey=("mix_ld", k0))
        xresb = [xar[:, 0:1024].rearrange("p (t c) -> p t c", t=4),
                 wsTb[:].rearrange("p a b -> p (a b)").bitcast(F32).rearrange("p (t c) -> p t c", t=4)]
        xresk = [["xarh", ("xar", 0), ("xar", 1)], ["wsTb"]]
        rsf = rsbc[:].rearrange("p a b -> p (a b)")
        ytb = [rsf[:, 0:1024].rearrange("p (t c) -> p t c", t=4), rsf[:, 1024:2048].rearrange("p (t c) -> p t c", t=4)]
        junk2 = t2[:].rearrange("p a b -> p (a b)")
        n_ = 0
        bankap = [psA[:, b_ * 512: b_ * 512 + 256] for b_ in range(4)] + [psB[:, b_ * 512: b_ * 512 + 256] for b_ in range(4)]
        bankk = [A_(b_) for b_ in range(4)] + [B_(b_) for b_ in range(4)]
        for cbk in range(16):
            sA = wq.get(idx[("o", cbk, 0)], ahead=1)
            sB = wq.get(idx[("o", cbk, 1)], ahead=1)
            for hh in range(2):
                xsrc = xs_d[NBLK - 1, hh * 512:(hh + 1) * 512, cbk * 256:(cbk + 1) * 256].rearrange("(t p) c -> p t c", p=128)
                P.dma("sp", lambda e, hh=hh, xsrc=xsrc: e.dma_start(out=xresb[hh], in_=xsrc), writes=xresk[hh], semkey=("xres", hh))

            def mm(e, kk, tt, sA=sA, sB=sB):
                sl = sA if kk < 32 else sB
                return e.matmul(bankap[tt], lhsT=mix(kk)[:, tt * 128:(tt + 1) * 128], rhs=wbuf[:, sl, kk % 32, :],
                                start=(kk == 0), stop=(kk == 63))
            for tt in range(8):
                P.op("pe", lambda e, tt=tt, mm=mm: mm(e, 0, tt), reads=["w%d" % sA] + HK, writes=[bankk[tt]])

            def fa(e, mm=mm):
                ins = None
                for kk in range(1, 32):
                    for tt in range(8):
                        ins = mm(e, kk, tt)
                return ins
            P.op("pe", fa, reads=["w%d" % sA] + HK, writes=bankk)

            def fb(e, mm=mm):
                ins = None
                for kk in range(32, 63):
                    for tt in range(8):
                        ins = mm(e, kk, tt)
                return ins
            P.op("pe", fb, reads=["w%d" % sB] + SCK + YBK, writes=bankk)
            for tt in range(8):
                P.op("pe", lambda e, tt=tt, mm=mm: mm(e, 63, tt), reads=["w%d" % sB] + YBK, writes=[bankk[tt]])
            for hh in range(2):
                xres, yt = xresb[hh], ytb[hh]
                ytk = [("ytb", hh)] + (["rsbc"] if cbk == 0 else [])
                ydst = y_d[hh * 512:(hh + 1) * 512, cbk * 256:(cbk + 1) * 256].rearrange("(t p) c -> p t c", p=128)
                psv = (psA if hh == 0 else psB)[:, :].rearrange("p (b c) -> p b c", b=4)[:, :, 0:256]
                P.op("dve", lambda e, xres=xres, yt=yt, psv=psv: e.tensor_tensor(out=yt, in0=psv, in1=xres, op=ALU.add),
                     reads=bankk[hh * 4:(hh + 1) * 4] + xresk[hh], writes=ytk)
                for tq in range(4):
                    tt = hh * 4 + tq
                    P.op("act", lambda e, tq=tq, tt=tt, yt=yt, cbk=cbk: e.activation(out=junk2, in_=yt[:, tq, :], func=AF.Square,
                                                                                    accum_out=ssqp[:, tt, cbk:cbk + 1]),
                         reads=[("ytb", hh), "ssqp"], writes=[("t2", 0), ("t2", 1), ("ssqp", tt, cbk)])
                P.dma("sp", lambda e, yt=yt, ydst=ydst: e.dma_start(out=ydst, in_=yt),
                      reads=[("ytb", hh)], writes=[("y_d", hh, cbk)], semkey=("yst", hh))

        allq = [("ssqp", a_, b_) for a_ in range(8) for b_ in range(16)]
        P.op("dve", lambda e: e.reduce_sum(out=fst[:, 0, :], in_=ssqp[:], axis=mybir.AxisListType.X), reads=allq + ["ssqp"], writes=["fst0"])
        P.op("act", lambda e: e.activation(out=fst[:, 1, :], in_=fst[:, 0, :], func=AF.Sqrt, scale=1.0 / D, bias=EPS), reads=["fst0"], writes=["fst1"])
        P.op("dve", lambda e: e.reciprocal(out=fst[:, 1, :], in_=fst[:, 1, :]), reads=["fst1"], writes=["fst1"])
        toks = []
        fgb = gbx[:]
        P.dma("sp", lambda e: e.dma_start(out=fgb, in_=fgbc_d[:, :]),
              writes=["gbx", "rsbc", "wsTb", "bspg", ("t2", 0), ("t2", 1), ("ytb", 0), ("ytb", 1)], semkey="fg_ld")
        for tt in range(8):
            pb = tt % 3
            keys = [S_(4 * pb + i) for i in range(4)]
            yt4 = SC[:, 4 * pb: 4 * pb + 4, :].rearrange("p a b -> p (a b)")
            half = tt // 4
            P.dma("sp", lambda e, yt4=yt4, tt=tt: e.dma_start(out=yt4, in_=y_d[tt * 128:(tt + 1) * 128, :]),
                  reads=[("y_d", half, c_) for c_ in range(16)], writes=keys, semkey=("y4", pb))
            P.op("dve", lambda e, yt4=yt4, tt=tt: e.scalar_tensor_tensor(out=yt4, in0=yt4, scalar=fst[:, 1, tt:tt + 1], in1=fgb,
                                                                       op0=ALU.mult, op1=ALU.mult),
                 reads=keys + ["fst1", "gbx"], writes=keys)
            toks.append(P.dma("act", lambda e, yt4=yt4, tt=tt: e.dma_start(out=out_d[tt * 128:(tt + 1) * 128, :], in_=yt4),
                              reads=keys, writes=[("out", tt)], semkey=("ost", pb)))
        P.finish("sp", toks)

        with nc.Block() as block:
            P.emit(block)
    return nc


_NC_CACHE = {}


def kernel(x, norm_g, w_in, conv_w, conv_b, w_gate_a, b_gate_a, w_gate_x, b_gate_x,
           lru_lambda, ln_v_g, ln_v_b, w_spatial, b_spatial, w_out, final_g):
    f = np.float32
    x2 = np.asarray(x, f).reshape(NBLK * TB, D)

    def cm(v):
        return np.ascontiguousarray(np.asarray(v, f).reshape(32, 128).T)
    par = np.stack([cm(conv_b[0]), cm(b_gate_a[0].reshape(-1)), cm(b_gate_x[0].reshape(-1)), cm(lru_lambda[0]),
                    cm(ln_v_g[0]), cm(ln_v_b[0]), cm(norm_g[0])], axis=1).reshape(128, 7 * 32)
    cw = np.ascontiguousarray(np.asarray(conv_w[0], f).reshape(4, 32, 128).transpose(2, 1, 0)).reshape(128, 128)
    shared = {
        "w_in": np.ascontiguousarray(np.asarray(w_in[0], f)),
        "w_out": np.ascontiguousarray(np.asarray(w_out[0], f)),
        "fgbc": np.ascontiguousarray(np.broadcast_to(np.asarray(final_g, f)[None, :], (128, D))),
        "ngbc": np.ascontiguousarray(np.broadcast_to(np.asarray(norm_g[0], f)[None, :], (128, D))),
        "cw": cw,
        "par": np.ascontiguousarray(par),
        "wga": np.ascontiguousarray(np.asarray(w_gate_a[0], f)),
        "wgx": np.ascontiguousarray(np.asarray(w_gate_x[0], f)),
        "wsT": np.ascontiguousarray(np.asarray(w_spatial[0], f).transpose(2, 0, 1)).reshape(128, 16 * 128),
        "bspbc": np.ascontiguousarray(np.broadcast_to(np.asarray(b_spatial[0], f).reshape(1, 16 * 128), (128, 16 * 128))),
    }
    in_maps = []
    for c in range(NCORES):
        xs = np.zeros((NBLK, TB, D), f)
        xs[NBLK - 1 - c:] = x2[: (c + 1) * TB].reshape(c + 1, TB, D)
        cmask = np.zeros((128, NBLK), f)
        cmask[:, NBLK - c:] = 1.0
        m = dict(shared)
        m["xs"] = xs
        m["cmask"] = cmask
        in_maps.append(m)
    if "nc" not in _NC_CACHE:
        _NC_CACHE["nc"] = build_nc()
    res = run_bass_kernel_spmd(_NC_CACHE["nc"], in_maps, core_ids=list(range(NCORES)))
    out = np.concatenate([np.asarray(r["out"], f) for r in res.results], axis=0)
    return out.reshape(1, NBLK * TB, D)
```

```python
import numpy as np
from contextlib import ExitStack
import concourse.bass as bass
import concourse.mybir as mybir
from concourse.bass_utils import run_bass_kernel_spmd

F32 = mybir.dt.float32
BF16 = mybir.dt.bfloat16
AF = mybir.ActivationFunctionType
ALU = mybir.AluOpType

NCORES = 8
D = 4096
TB = 1024
NBLK = 8
KT = D // 128
EPS = 1e-6


class Prog:
    ENG = ("pe", "act", "dve", "pool", "sp")

    def __init__(self, nc, stack):
        self.nc = nc
        self.stack = stack
        self.ops = {e: [] for e in self.ENG}
        self.esem = {e: stack.enter_context(nc.semaphore("S_" + e)) for e in self.ENG}
        self.ecount = {e: 0 for e in self.ENG}
        self.dsem = {}
        self.lastw = {}
        self.readers = {}
        self.waited = {e: {} for e in self.ENG}

    def _need(self, eng, toks):
        best = {}
        for t in toks:
            if t is None:
                continue
            sem, val, src = t
            if src == eng and eng == "pe":
                continue
            k = id(sem)
            if k not in best or best[k][1] < val:
                best[k] = (sem, val)
        out = []
        w = self.waited[eng]
        for k, (sem, val) in best.items():
            if k in w and w[k] >= val:
                continue
            w[k] = val
            out.append((sem, val))
        return out

    def _deps(self, reads, writes):
        toks = []
        for k in reads:
            toks.append(self.lastw.get(k))
        for k in writes:
            toks.append(self.lastw.get(k))
            toks.extend(self.readers.get(k, []))
        return toks

    def _commit(self, tok, reads, writes):
        for k in reads:
            self.readers.setdefault(k, []).append(tok)
        for k in writes:
            self.lastw[k] = tok
            self.readers[k] = []

    def op(self, eng, fn, reads=(), writes=()):
        waits = self._need(eng, self._deps(reads, writes))
        self.ecount[eng] += 1
        tok = (self.esem[eng], self.ecount[eng], eng)
        self.ops[eng].append((waits, fn, (self.esem[eng], 1)))
        self._commit(tok, reads, writes)
        return tok

    def dma(self, eng, fn, reads=(), writes=(), semkey=None):
        if semkey is None:
            semkey = ("dma",) + tuple(writes if writes else reads)
        if semkey not in self.dsem:
            self.dsem[semkey] = [self.stack.enter_context(
                self.nc.semaphore("D%d" % len(self.dsem))), 0]
        ent = self.dsem[semkey]
        prev = (ent[0], ent[1], "dma") if ent[1] > 0 else None
        waits = self._need(eng, self._deps(reads, writes) + [prev])
        ent[1] += 16
        tok = (ent[0], ent[1], "dma")
        self.ops[eng].append((waits, fn, (ent[0], 16)))
        self._commit(tok, reads, writes)
        return tok

    def finish(self, eng, toks):
        waits = self._need(eng, toks)
        self.ops[eng].append((waits, None, None))

    def emit(self, block):
        names = {"pe": "tensor", "act": "scalar", "dve": "vector", "pool": "gpsimd", "sp": "sync"}
        for e in self.ENG:
            lst = self.ops[e]

            def body(engine, lst=lst):
                for waits, fn, inc in lst:
                    for sem, val in waits:
                        engine.wait_ge(sem, val)
                    if fn is None:
                        continue
                    ins = fn(engine)
                    if inc is not None:
                        ins.then_inc(inc[0], inc[1])
            getattr(block, names[e])(body)


def build_nc():
    nc = bass.Bass("TRN2", target_bir_lowering=False)

    def din(n, s):
        return nc.dram_tensor(n, s, F32, kind="ExternalInput").ap()

    xs_d = din("xs", [NBLK, TB, D])
    cmask_d = din("cmask", [128, NBLK])
    w_in_d = din("w_in", [D, 20480])
    w_out_d = din("w_out", [8192, D])
    fgbc_d = din("fgbc", [128, D])
    ngbc_d = din("ngbc", [128, D])
    cw_d = din("cw", [128, 32 * 4])
    par_d = din("par", [128, 7 * 32])
    wga_d = din("wga", [16, 256, 256])
    wgx_d = din("wgx", [16, 256, 256])
    wsT_d = din("wsT", [128, 16 * 128])
    bsp_d = din("bspbc", [128, 16 * 128])
    out_d = nc.dram_tensor("out", [TB, D], F32, kind="ExternalOutput").ap()
    mixed_d = nc.dram_tensor("mixed_d", [64, 128, TB], BF16, kind="Internal").ap()
    gv_d = nc.dram_tensor("gv_d", [16, 128, 2048], F32, kind="Internal").ap()
    y_d = nc.dram_tensor("y_d", [TB, D], F32, kind="Internal").ap()

    w_in_v = w_in_d.rearrange("(kt p) c -> p kt c", p=128)
    w_out_v = w_out_d.rearrange("(kk p) c -> p kk c", p=128)

    st = ExitStack()
    with st:
        def sb(n, s, d=F32):
            return st.enter_context(nc.sbuf_tensor(n, s, d))
        big = sb("big", [128, 32768], BF16)
        wbuf = sb("wbuf", [128, 3, KT, 256], BF16)
        SC = sb("SC", [128, 14, 1024], F32)
        xar = sb("xar", [128, 1027], F32)
        ybf = sb("ybf", [128, 4096], BF16)
        wg = sb("wg", [128, 2, 2, 2, 256], BF16)
        mxo = sb("mxo", [128, 1024], BF16)
        gbx = sb("gbx", [128, D], F32)
        rsbc = gbx[:, 0:2048].rearrange("p (a b) -> p a b", a=16)
        wsTb = gbx[:, 2048:3072].bitcast(BF16).rearrange("p (a b) -> p a b", a=16)
        t2 = gbx[:, 3072:3328].rearrange("p (a b) -> p a b", a=2)
        bspg = gbx[:, 3328:3456]
        cw = sb("cw_s", [128, 32, 4], F32)
        par = sb("par_s", [128, 7, 32], F32)
        c8 = sb("c8", [128, 32], F32)
        c16 = sb("c16", [128, 32], F32)
        cmask = sb("cmask_s", [128, NBLK], F32)
        hstate = sb("hstate", [128, 32], F32)
        halo = sb("halo", [128, 32, 3], F32)
        initt = sb("initt", [128, 2], F32)
        identb = sb("identb", [128, 128], BF16)
        ssq0 = sb("ssq0", [128, 64], F32)
        rstd0 = sb("rstd0", [128, 64], F32)
        s1p = sb("s1p", [128, 8, 16], F32)
        s2p = sb("s2p", [128, 8, 16], F32)
        lnst = sb("lnst", [128, 5, 8], F32)
        ssqp = sb("ssqp", [128, 8, 16], F32)
        fst = sb("fst", [128, 2, 8], F32)
        psA = st.enter_context(nc.psum_tensor("psA", [128, 2048], F32))
        psB = st.enter_context(nc.psum_tensor("psB", [128, 2048], F32))

        hnT = big[:].rearrange("p (k t) -> p k t", k=KT)
        mixh = big[:].rearrange("p (k t) -> p k t", k=64)
        ybf3 = ybf[:].rearrange("p (q c t) -> p q c t", q=2, c=2)
        zb = ybf[:, 0:2048].rearrange("p (t c) -> p t c", t=8)
        CB, BGA, BGX, LAM, LNG, LNB, NG = range(7)

        P = Prog(nc, st)
        A_ = lambda k: "psA%d" % k
        B_ = lambda k: "psB%d" % k
        S_ = lambda k: "S%d" % k

        P.dma("sp", lambda e: e.dma_start(out=cw[:].rearrange("p a b -> p (a b)"), in_=cw_d[:, :]), writes=["cw"])
        P.dma("sp", lambda e: e.dma_start(out=par[:].rearrange("p a b -> p (a b)"), in_=par_d[:, :]), writes=["par"])
        P.dma("sp", lambda e: e.dma_start(out=cmask[:], in_=cmask_d[:, :]), writes=["cmask"])
        P.dma("sp", lambda e: e.dma_start(out=gbx[:], in_=ngbc_d[:, :]), writes=["gbx"])
        identf = SC[:, 13, 0:128]
        P.op("pool", lambda e: e.memset(identf, 0.0), writes=[S_(13)])
        P.op("pool", lambda e: e.affine_select(out=identf, in_=identf, compare_op=ALU.not_equal,
                                               fill=1.0, base=0, pattern=[[-1, 128]], channel_multiplier=1),
             reads=[S_(13)], writes=[S_(13)])
        P.op("pool", lambda e: e.tensor_copy(out=identb[:], in_=identf), reads=[S_(13)], writes=["ident"])
        P.op("dve", lambda e: e.memset(hstate[:], 0.0), writes=["hstate"])
        P.op("dve", lambda e: e.memset(halo[:].rearrange("p a b -> p (a b)"), 0.0), writes=["halo"])
        for ap_, nm in [(ssq0[:], "ssq0"), (s1p[:].rearrange("p a b -> p (a b)"), "s1p"),
                        (s2p[:].rearrange("p a b -> p (a b)"), "s2p"), (ssqp[:].rearrange("p a b -> p (a b)"), "ssqp")]:
            P.op("dve", lambda e, ap_=ap_: e.memset(ap_, 0.0), writes=[nm])
        P.op("act", lambda e: e.activation(out=c8[:], in_=par[:, LAM, :], func=AF.Exp, scale=-1.0), reads=["par"], writes=["c8"])
        P.op("act", lambda e: e.activation(out=c8[:], in_=c8[:], func=AF.Ln, bias=1.0, scale=1.0), reads=["c8"], writes=["c8"])
        P.op("dve", lambda e: e.tensor_scalar(out=c16[:], in0=c8[:], scalar1=-16.0, scalar2=None, op0=ALU.mult), reads=["c8"], writes=["c16"])
        P.op("dve", lambda e: e.tensor_scalar(out=c8[:], in0=c8[:], scalar1=-8.0, scalar2=None, op0=ALU.mult), reads=["c8", "c16"], writes=["c8"])
        wstate = {"n": 0}

        def wload(src_ap):
            slot = wstate["n"] % 3
            wstate["n"] += 1
            if isinstance(src_ap, tuple):
                dst = wbuf[:, slot].rearrange("p k c -> p (k c)").bitcast(F32)
                P.dma("pool", lambda e, dst=dst, ap=src_ap[1]: e.dma_start(out=dst, in_=ap), writes=["w%d" % slot])
            else:
                P.dma("pool", lambda e, slot=slot, src_ap=src_ap: e.dma_start(out=wbuf[:, slot], in_=src_ap),
                      writes=["w%d" % slot])
            return slot

        def wcols(c0):
            return w_in_v[:, :, c0:c0 + 256]

        class WQ:
            def __init__(self, srcs):
                self.srcs = srcs
                self.slots = {}
                self.next = 0

            def prefetch(self, upto):
                while self.next <= min(upto, len(self.srcs) - 1):
                    self.slots[self.next] = wload(self.srcs[self.next])
                    self.next += 1

            def get(self, i, ahead=2):
                self.prefetch(i + ahead)
                return self.slots[i]

        srcs = []
        idx = {}
        for j in range(NBLK - 1):
            if j >= 1:
                idx[("x0", j)] = len(srcs); srcs.append(("x", xs_d[j, 0:128, :]))
            for h in range(16):
                idx[("xa", j, h)] = len(srcs); srcs.append(wcols(h * 256))
        idx[("x0", NBLK - 1)] = len(srcs); srcs.append(("x", xs_d[NBLK - 1, 0:128, :]))
        idx[("xa", NBLK - 1, 0)] = len(srcs); srcs.append(wcols(0))
        for h in range(16):
            if h + 1 < 16:
                idx[("xa", NBLK - 1, h + 1)] = len(srcs); srcs.append(wcols((h + 1) * 256))
            idx[("ga", h)] = len(srcs); srcs.append(wcols(4096 + h * 256))
        for g in range(16):
            idx[("v", g)] = len(srcs); srcs.append(wcols(3 * 4096 + g * 256))
        for g in range(16):
            idx[("u", g)] = len(srcs); srcs.append(wcols(2 * 4096 + g * 256))
            idx[("gb", g)] = len(srcs); srcs.append(wcols(4 * 4096 + g * 256))
        for cbk in range(16):
            idx[("o", cbk, 0)] = len(srcs); srcs.append(w_out_v[:, 0:32, cbk * 256:(cbk + 1) * 256])
            idx[("o", cbk, 1)] = len(srcs); srcs.append(w_out_v[:, 32:64, cbk * 256:(cbk + 1) * 256])
        wq = WQ(srcs)

        def proj_cm(slot, bank0=0):
            for ct in range(2):
                for th in range(2):
                    bk = ct * 2 + th

                    def fn(e, ct=ct, th=th, bk=bk):
                        ins = None
                        for kt in range(KT):
                            ins = e.matmul(psA[:, bk * 512:(bk + 1) * 512], lhsT=wbuf[:, slot, kt, ct * 128:(ct + 1) * 128],
                                           rhs=hnT[:, kt, th * 512:(th + 1) * 512], start=(kt == 0), stop=(kt == KT - 1))
                        return ins
                    P.op("pe", fn, reads=["w%d" % slot, ("hnT", "d"), ("hnT", "a")], writes=[A_(bk)])

        for j in range(NBLK):
            own = (j == NBLK - 1)
            YB4 = [("ybf", 0, 0), ("ybf", 0, 1), ("ybf", 1, 0), ("ybf", 1, 1)]
            xn = ybf[:, :]
            junk = SC[:, 12:14, :].rearrange("p a b -> p (a b)").bitcast(BF16)

            x0slot = wq.get(idx[("x0", j)]) if ("x0", j) in idx else None

            def p0_xt(tt, j=j, x0slot=x0slot):
                if tt == 0 and x0slot is not None:
                    return None, ["w%d" % x0slot], wbuf[:, x0slot].rearrange("p k c -> p (k c)").bitcast(F32)
                xb = (j * 8 + tt) % 3
                return xb, [S_(4 * xb + i) for i in range(4)], SC[:, 4 * xb:4 * xb + 4, :].rearrange("p a b -> p (a b)")

            def p0_fa(tt, j=j):
                xb, xkeys, xt = p0_xt(tt)
                col = j * 8 + tt
                if xb is not None:
                    P.dma("sp", lambda e, xt=xt, j=j, tt=tt: e.dma_start(out=xt, in_=xs_d[j, tt * 128:(tt + 1) * 128, :]),
                          writes=xkeys, semkey=("xt", xb))
                P.op("act", lambda e, xt=xt, col=col: e.activation(out=junk, in_=xt, func=AF.Square, accum_out=ssq0[:, col:col + 1]),
                     reads=xkeys + ["ssq0"], writes=[S_(12), S_(13), ("ssq0", col)])
                P.op("act", lambda e, col=col: e.activation(out=rstd0[:, col:col + 1], in_=ssq0[:, col:col + 1], func=AF.Sqrt,
                                                            scale=1.0 / D, bias=EPS),
                     reads=[("ssq0", col)], writes=[("rstd0", col)])
                P.op("dve", lambda e, col=col: e.reciprocal(out=rstd0[:, col:col + 1], in_=rstd0[:, col:col + 1]),
                     reads=[("rstd0", col)], writes=[("rstd0", col)])

            def p0_stt(tt, j=j):
                xb, xkeys, xt = p0_xt(tt)
                col = j * 8 + tt
                P.op("dve", lambda e, xt=xt, col=col: e.scalar_tensor_tensor(out=xn, in0=xt, scalar=rstd0[:, col:col + 1], in1=gbx[:],
                                                                            op0=ALU.mult, op1=ALU.mult),
                     reads=xkeys + [("rstd0", col), "gbx"], writes=YB4)

            def p0_back(tt, j=j):
                ps = psA if tt % 2 == 0 else psB
                pk = A_ if tt % 2 == 0 else B_
                ps16 = ps[:, :].bitcast(BF16)
                for q in range(4):
                    def fn(e, q=q, ps16=ps16):
                        ins = None
                        for i in range(8):
                            kt = q * 8 + i
                            ins = e.transpose(out=ps16[:, q * 1024 + i * 128: q * 1024 + (i + 1) * 128],
                                              in_=xn[:, kt * 128:(kt + 1) * 128], identity=identb[:])
                        return ins
                    P.op("pe", fn, reads=YB4 + ["ident"], writes=[pk(q)])
                    src = ps16[:, q * 1024:(q + 1) * 1024].rearrange("p (a b) -> p a b", a=8)
                    dst = hnT[:, q * 8:(q + 1) * 8, tt * 128:(tt + 1) * 128]
                    if q == 0:
                        P.op("dve", lambda e, src=src, dst=dst: e.tensor_copy(out=dst, in_=src), reads=[pk(q)], writes=[("hnT", "d")])
                    else:
                        P.op("act", lambda e, src=src, dst=dst: e.copy(out=dst, in_=src), reads=[pk(q)], writes=[("hnT", "a")])

            p0_fa(0)
            p0_stt(0)
            for tt in range(8):
                if tt + 1 < 8:
                    p0_fa(tt + 1)
                p0_back(tt)
                if tt + 1 < 8:
                    p0_stt(tt + 1)

            def stage1(h, p, j=j):
                slot = wq.get(idx[("xa", j, h)])
                P.dma("pool", lambda e, p=p, h=h: e.dma_start(out=wg[:, p, 0], in_=wga_d[h].rearrange("(it p) j -> p it j", p=128)),
                      writes=[("wg", p, 0)])
                P.dma("pool", lambda e, p=p, h=h: e.dma_start(out=wg[:, p, 1], in_=wgx_d[h].rearrange("(it p) j -> p it j", p=128)),
                      writes=[("wg", p, 1)])
                proj_cm(slot)
                stage1_ct(h, p, 0)

            def stage1_ct(h, p, ct, j=j):
                if True:
                    ctg = h * 2 + ct
                    ysl = ct if p == 0 else 12 + ct
                    yk = S_(ysl)
                    y = SC[:, ysl, :]
                    P.op("dve", lambda e, ct=ct, ctg=ctg: e.tensor_copy(out=xar[:, 0:3], in_=halo[:, ctg, :]),
                         reads=["halo"], writes=["xarh"])
                    for th in range(2):
                        bk = ct * 2 + th
                        P.op("act", lambda e, ct=ct, th=th, bk=bk: e.copy(out=xar[:, 3 + th * 512: 3 + (th + 1) * 512],
                                                                          in_=psA[:, bk * 512:(bk + 1) * 512]),
                             reads=[A_(bk)], writes=[("xar", th)])
                    xk = ["xarh", ("xar", 0), ("xar", 1)]
                    P.op("dve", lambda e, ct=ct, ctg=ctg, y=y: e.tensor_scalar(out=y, in0=xar[:, 3:1027], scalar1=cw[:, ctg, 3:4],
                                                                             scalar2=par[:, CB, ctg:ctg + 1], op0=ALU.mult, op1=ALU.add),
                         reads=xk + ["cw", "par"], writes=[yk])
                    for k in range(3):
                        P.op("dve", lambda e, ct=ct, ctg=ctg, y=y, k=k: e.scalar_tensor_tensor(
                            out=y, in0=xar[:, k:k + 1024], scalar=cw[:, ctg, k:k + 1], in1=y, op0=ALU.mult, op1=ALU.add),
                            reads=xk + ["cw", yk], writes=[yk])
                    P.op("dve", lambda e, ct=ct, ctg=ctg: e.tensor_copy(out=halo[:, ctg, :], in_=xar[:, 1024:1027]),
                         reads=xk, writes=["halo"])
                    P.op("dve", lambda e, ct=ct, y=y, p=p: e.tensor_copy(out=ybf3[:, p, ct, :], in_=y), reads=[yk], writes=[("ybf", p, ct)])

            def stage2(h, p, j=j, own=own):
                for jt in range(2):
                    ctg = h * 2 + jt
                    base = 2 + 5 * jt
                    for gate in range(2):
                        for th in range(2):
                            bk = gate * 2 + th

                            def fn(e, gate=gate, jt=jt, th=th, bk=bk, p=p):
                                ins = None
                                for it in range(2):
                                    ins = e.matmul(psB[:, bk * 512:(bk + 1) * 512], lhsT=wg[:, p, gate, it, jt * 128:(jt + 1) * 128],
                                                   rhs=ybf3[:, p, it, th * 512:(th + 1) * 512], start=(it == 0), stop=(it == 1))
                                return ins
                            P.op("pe", fn, reads=[("wg", p, gate), ("ybf", p, 0), ("ybf", p, 1)], writes=[B_(bk)])
                    for gate in range(2):
                        dst = SC[:, base + (0 if gate == 0 else 2), :]
                        dk = S_(base + (0 if gate == 0 else 2))
                        bcol = BGA if gate == 0 else BGX
                        for th in range(2):
                            bk = gate * 2 + th
                            P.op("act", lambda e, dst=dst, th=th, bk=bk, bcol=bcol, ctg=ctg: e.activation(
                                out=dst[:, th * 512:(th + 1) * 512], in_=psB[:, bk * 512:(bk + 1) * 512], func=AF.Sigmoid,
                                bias=par[:, bcol, ctg:ctg + 1], scale=1.0), reads=[B_(bk), "par"], writes=[dk])

            def stage2_rest(h, p, j=j, own=own):
                for jt in range(2):
                    ctg = h * 2 + jt
                    base = 2 + 5 * jt
                    r_, a2_ = SC[:, base, :], SC[:, base + 1, :]
                    rk, a2k = S_(base), S_(base + 1)
                    P.op("act", lambda e, r_=r_, a2_=a2_, ctg=ctg: e.activation(out=a2_, in_=r_, func=AF.Exp, scale=c16[:, ctg:ctg + 1]),
                         reads=[rk, "c16"], writes=[a2k])
                    P.op("act", lambda e, r_=r_, ctg=ctg: e.activation(out=r_, in_=r_, func=AF.Exp, scale=c8[:, ctg:ctg + 1]),
                         reads=[rk, "c8"], writes=[rk])
                for jt in range(2):
                    base = 2 + 5 * jt
                    a2_, a2k = SC[:, base + 1, :], S_(base + 1)
                    P.op("dve", lambda e, a2_=a2_: e.tensor_scalar(out=a2_, in0=a2_, scalar1=1.0, scalar2=None, op0=ALU.min),
                         reads=[a2k], writes=[a2k])
                    P.op("act", lambda e, a2_=a2_: e.activation(out=a2_, in_=a2_, func=AF.Sqrt, scale=-1.0, bias=1.0),
                         reads=[a2k], writes=[a2k])
                for jt in range(2):
                    ctg = h * 2 + jt
                    base = 2 + 5 * jt
                    ysl = jt if p == 0 else 12 + jt
                    r_, a2_, i_, b_, ho_ = [SC[:, base + q, :] for q in range(5)]
                    rk, a2k, ik, bk_, hok = [S_(base + q) for q in range(5)]
                    P.op("dve", lambda e, a2_=a2_, i_=i_, b_=b_: e.tensor_tensor(out=b_, in0=a2_, in1=i_, op=ALU.mult),
                         reads=[a2k, ik], writes=[bk_])
                    P.op("dve", lambda e, b_=b_, ysl=ysl: e.tensor_tensor(out=b_, in0=b_, in1=SC[:, ysl, :], op=ALU.mult),
                         reads=[bk_, S_(ysl)], writes=[bk_])
                    P.op("dve", lambda e, ctg=ctg, jt=jt, j=j: e.tensor_tensor(out=initt[:, jt:jt + 1], in0=hstate[:, ctg:ctg + 1],
                                                                             in1=cmask[:, j:j + 1], op=ALU.mult),
                         reads=["hstate", "cmask"], writes=[("initt", jt)])
                    P.op("dve", lambda e, r_=r_, b_=b_, ho_=ho_, jt=jt: e.tensor_tensor_scan(
                        out=ho_, data0=r_, data1=b_, initial=initt[:, jt:jt + 1], op0=ALU.mult, op1=ALU.add),
                        reads=[rk, bk_, ("initt", jt)], writes=[hok])
                    P.op("dve", lambda e, ho_=ho_, ctg=ctg: e.tensor_copy(out=hstate[:, ctg:ctg + 1], in_=ho_[:, 1023:1024]),
                         reads=[hok], writes=["hstate"])
                if own:
                    slot2 = wq.get(idx[("ga", h)])
                    proj_cm(slot2)
                    for ct in range(2):
                        ctg = h * 2 + ct
                        base = 2 + 5 * ct
                        sg = SC[:, base, :]
                        sgk = [S_(base)]
                        for th in range(2):
                            bk = ct * 2 + th
                            P.op("act", lambda e, sg=sg, th=th, bk=bk: e.activation(out=sg[:, th * 512:(th + 1) * 512],
                                                                                   in_=psA[:, bk * 512:(bk + 1) * 512], func=AF.Silu),
                                 reads=[A_(bk)], writes=sgk)
                        P.op("dve", lambda e, sg=sg, base=base: e.tensor_tensor(out=mxo[:], in0=SC[:, base + 4, :], in1=sg, op=ALU.mult),
                             reads=sgk + [S_(base + 4)], writes=["mxo"])
                        P.dma("sp", lambda e, ctg=ctg: e.dma_start(out=mixed_d[ctg, :, :], in_=mxo[:]),
                              reads=["mxo"], writes=[("mixed_d", ctg)], semkey="mxo_st")

            stage1(0, 0)
            stage1_ct(0, 0, 1)
            for h in range(16):
                if h + 1 < 16:
                    stage1(h + 1, (h + 1) % 2)
                stage2(h, h % 2)
                if h + 1 < 16:
                    stage1_ct(h + 1, (h + 1) % 2, 1)
                stage2_rest(h, h % 2)


        P.op("dve", lambda e: e.memset(t2[:].rearrange("p a b -> p (a b)"), 0.0),
             writes=["gbx", "rsbc", "wsTb", "bspg", ("t2", 0), ("t2", 1)])
        wsTf = SC[:, 5:7, :].rearrange("p a b -> p (a b)")
        wsTf3 = wsTf.rearrange("p (g i) -> p g i", g=16)
        onesv = SC[:, 7, 0:128]
        P.dma("sp", lambda e: e.dma_start(out=wsTf, in_=wsT_d[:, :]), writes=[S_(5), S_(6)])
        P.op("dve", lambda e: e.memset(onesv, 1.0), writes=[S_(7)])
        P.op("dve", lambda e: e.memset(wsTf3[64:128, :, 0:64], 0.0), reads=[], writes=[S_(5), S_(6)])
        P.op("dve", lambda e: e.tensor_copy(out=wsTb[:].rearrange("p a b -> p (a b)"), in_=wsTf), reads=[S_(5), S_(6)], writes=["wsTb"])

        for g in range(16):
            slot = wq.get(idx[("v", g)])
            sb0 = 0 if g % 2 == 0 else 3
            stg = SC[:, sb0:sb0 + 2, :].rearrange("p a b -> p (a b)").rearrange("p (t c) -> p t c", t=8)
            for tt in range(8):
                bk = tt % 4

                def fn(e, tt=tt, bk=bk, slot=slot):
                    ins = None
                    for kt in range(KT):
                        ins = e.matmul(psA[:, bk * 512: bk * 512 + 256], lhsT=hnT[:, kt, tt * 128:(tt + 1) * 128],
                                       rhs=wbuf[:, slot, kt, :], start=(kt == 0), stop=(kt == KT - 1))
                    return ins
                P.op("pe", fn, reads=["w%d" % slot, ("hnT", "d"), ("hnT", "a")], writes=[A_(bk)])
                P.op("act", lambda e, tt=tt, bk=bk, g=g, stg=stg: e.activation(out=stg[:, tt, :], in_=psA[:, bk * 512: bk * 512 + 256],
                                                                              func=AF.Gelu, accum_out=s1p[:, tt, g:g + 1]),
                     reads=[A_(bk), "s1p"], writes=[S_(sb0 + tt // 4), ("s1p", tt, g)])
                P.op("act", lambda e, tt=tt, g=g, stg=stg: e.activation(out=SC[:, 2, 0:256], in_=stg[:, tt, :], func=AF.Square,
                                                                       accum_out=s2p[:, tt, g:g + 1]),
                     reads=[S_(sb0 + tt // 4), "s2p"], writes=[S_(2), ("s2p", tt, g)])
            P.dma("sp", lambda e, g=g, sb0=sb0: e.dma_start(out=gv_d[g, :, :], in_=SC[:, sb0:sb0 + 2, :].rearrange("p a b -> p (a b)")),
                  reads=[S_(sb0), S_(sb0 + 1)], writes=[("gv_d", g)],
                  semkey=("gv_st", g % 2))
        for q in range(4):
            P.op("pe", lambda e, q=q: e.matmul(psA[:, q * 512:(q + 1) * 512], lhsT=onesv, rhs=wsTf[:, q * 512:(q + 1) * 512],
                                               start=True, stop=True), reads=[S_(7), S_(5), S_(6)], writes=[A_(q)])
            P.op("act", lambda e, q=q: e.copy(out=rsbc[:].rearrange("p a b -> p (a b)")[:, q * 512:(q + 1) * 512],
                                              in_=psA[:, q * 512:(q + 1) * 512]), reads=[A_(q)], writes=["rsbc"])
        allp = [("s1p", t_, g_) for t_ in range(8) for g_ in range(16)] + [("s2p", t_, g_) for t_ in range(8) for g_ in range(16)]
        P.op("dve", lambda e: e.reduce_sum(out=lnst[:, 0, :], in_=s1p[:], axis=mybir.AxisListType.X), reads=allp + ["s1p"], writes=["ln0"])
        P.op("dve", lambda e: e.reduce_sum(out=lnst[:, 1, :], in_=s2p[:], axis=mybir.AxisListType.X), reads=allp + ["s2p"], writes=["ln1"])
        P.op("dve", lambda e: e.tensor_scalar(out=lnst[:, 0, :], in0=lnst[:, 0, :], scalar1=1.0 / D, scalar2=None, op0=ALU.mult),
             reads=["ln0"], writes=["ln0"])
        P.op("dve", lambda e: e.tensor_tensor(out=lnst[:, 2, :], in0=lnst[:, 0, :], in1=lnst[:, 0, :], op=ALU.mult),
             reads=["ln0"], writes=["ln2"])
        P.op("dve", lambda e: e.scalar_tensor_tensor(out=lnst[:, 1, :], in0=lnst[:, 1, :], scalar=1.0 / D, in1=lnst[:, 2, :],
                                                     op0=ALU.mult, op1=ALU.subtract), reads=["ln1", "ln2"], writes=["ln1"])
        P.op("act", lambda e: e.activation(out=lnst[:, 3, :], in_=lnst[:, 1, :], func=AF.Sqrt, scale=1.0, bias=EPS),
             reads=["ln1"], writes=["ln3"])
        P.op("dve", lambda e: e.reciprocal(out=lnst[:, 3, :], in_=lnst[:, 3, :]), reads=["ln3"], writes=["ln3"])
        P.op("dve", lambda e: e.scalar_tensor_tensor(out=lnst[:, 4, :], in0=lnst[:, 0, :], scalar=-1.0, in1=lnst[:, 3, :],
                                                     op0=ALU.mult, op1=ALU.mult), reads=["ln0", "ln3"], writes=["ln4"])

        for g in range(16):
            gvg = SC[:, 10:12, :].rearrange("p a b -> p (a b)").rearrange("p (t c) -> p t c", t=8)
            P.dma("sp", lambda e, g=g: e.dma_start(out=SC[:, 10:12, :].rearrange("p a b -> p (a b)"), in_=gv_d[g, :, :]),
                  reads=[("gv_d", g)], writes=[S_(10), S_(11)], semkey="gv_ld")
            for tt in range(8):
                P.op("dve", lambda e, tt=tt, gvg=gvg: e.tensor_scalar(out=zb[:, tt, :], in0=gvg[:, tt, :], scalar1=lnst[:, 3, tt:tt + 1],
                                                                    scalar2=lnst[:, 4, tt:tt + 1], op0=ALU.mult, op1=ALU.add),
                     reads=[S_(10), S_(11), "ln3", "ln4"], writes=[("ybf", 0, 0), ("ybf", 0, 1)])
            P.dma("sp", lambda e, g=g: e.dma_start(out=bspg[:], in_=bsp_d[:, g * 128:(g + 1) * 128]), writes=["bspg"])
            slot_u = wq.get(idx[("u", g)])
            proj_cm(slot_u)
            for ct in range(2):
                for th in range(2):
                    bk = ct * 2 + th
                    P.op("act", lambda e, ct=ct, th=th, bk=bk: e.activation(out=SC[:, ct, th * 512:(th + 1) * 512],
                                                                          in_=psA[:, bk * 512:(bk + 1) * 512], func=AF.Gelu),
                         reads=[A_(bk)], writes=[S_(ct)])
            slot_g = wq.get(idx[("gb", g)])
            proj_cm(slot_g)
            for ct in range(2):
                for th in range(2):
                    bk = ct * 2 + th
                    P.op("act", lambda e, ct=ct, th=th, bk=bk: e.activation(out=SC[:, 2 + ct, th * 512:(th + 1) * 512],
                                                                          in_=psA[:, bk * 512:(bk + 1) * 512], func=AF.Silu),
                         reads=[A_(bk)], writes=[S_(2 + ct)])
            for ct in range(2):
                ctg = g * 2 + ct

                def fn(e, ct=ct, g=g):
                    ins = None
                    for tt in range(8):
                        ins = e.matmul(psB[:, ct * 1024 + tt * 128: ct * 1024 + (tt + 1) * 128], lhsT=zb[:, tt, ct * 128:(ct + 1) * 128],
                                       rhs=wsTb[:, g, :], start=True, stop=True)
                    return ins
                P.op("pe", fn, reads=[("ybf", 0, 0), ("ybf", 0, 1), "wsTb"], writes=[B_(2 * ct), B_(2 * ct + 1)])
                P.op("dve", lambda e, ct=ct, ctg=ctg, g=g: e.scalar_tensor_tensor(out=t2[:, ct, :], in0=rsbc[:, g, :],
                                                                                scalar=par[:, LNB, ctg:ctg + 1], in1=bspg[:],
                                                                                op0=ALU.mult, op1=ALU.add),
                     reads=["rsbc", "par", "bspg"], writes=[("t2", ct)])
                sk = S_(4 + ct)
                s_ = SC[:, 4 + ct, :]
                for tt in range(8):
                    P.op("dve", lambda e, ct=ct, ctg=ctg, tt=tt, s_=s_: e.scalar_tensor_tensor(
                        out=s_[:, tt * 128:(tt + 1) * 128], in0=psB[:, ct * 1024 + tt * 128: ct * 1024 + (tt + 1) * 128],
                        scalar=par[:, LNG, ctg:ctg + 1], in1=t2[:, ct, :], op0=ALU.mult, op1=ALU.add),
                        reads=[B_(2 * ct + tt // 4), ("t2", ct), "par"], writes=[sk])
                sks = [sk]
                P.op("dve", lambda e, ct=ct, s_=s_: e.tensor_tensor(out=s_, in0=s_, in1=SC[:, ct, :], op=ALU.mult),
                     reads=sks + [S_(ct)], writes=sks)
                P.op("dve", lambda e, ct=ct, s_=s_: e.tensor_tensor(out=mxo[:], in0=s_, in1=SC[:, 2 + ct, :], op=ALU.mult),
                     reads=sks + [S_(2 + ct)], writes=["mxo"])
                P.dma("sp", lambda e, ctg=ctg: e.dma_start(out=mixed_d[32 + ctg, :, :], in_=mxo[:]),
                      reads=["mxo"], writes=[("mixed_d", 32 + ctg)], semkey="mxo_st")

        mixed_v = mixed_d.rearrange("k p t -> p k t")
        scv = SC[:].rearrange("p a b -> p (a b)").bitcast(BF16).rearrange("p (k t) -> p k t", k=28)
        ybv = ybf[:].rearrange("p (k t) -> p k t", k=4)
        SCK = [S_(k_) for k_ in range(14)]
        YBK = [("ybf", 0, 0), ("ybf", 0, 1), ("ybf", 1, 0), ("ybf", 1, 1)]
        HK = [("hnT", "d"), ("hnT", "a")]
        MIXK = HK + SCK + YBK

        def mix(kk):
            if kk < 32:
                return hnT[:, kk, :]
            if kk < 60:
                return scv[:, kk - 32, :]
            return ybv[:, kk - 60, :]
        for (k0, k1, dst, wk) in [(0, 16, hnT[:, 0:16, :], HK), (16, 32, hnT[:, 16:32, :], HK), (32, 48, scv[:, 0:16, :], SCK),
                                  (48, 60, scv[:, 16:28, :], SCK), (60, 64, ybv[:, :, :], YBK)]:
            P.dma("pool" if k0 < 32 else "sp", lambda e, k0=k0, k1=k1, dst=dst: e.dma_start(out=dst, in_=mixed_v[:, k0:k1, :]),
                  reads=[("mixed_d", k_) for k_ in range(k0, k1)], writes=wk, semkey=("mix_ld", k0))
        xresb = [xar[:, 0:1024].rearrange("p (t c) -> p t c", t=4),
                 wsTb[:].rearrange("p a b -> p (a b)").bitcast(F32).rearrange("p (t c) -> p t c", t=4)]
        xresk = [["xarh", ("xar", 0), ("xar", 1)], ["wsTb"]]
        rsf = rsbc[:].rearrange("p a b -> p (a b)")
        ytb = [rsf[:, 0:1024].rearrange("p (t c) -> p t c", t=4), rsf[:, 1024:2048].rearrange("p (t c) -> p t c", t=4)]
        junk2 = t2[:].rearrange("p a b -> p (a b)")
        n_ = 0
        bankap = [psA[:, b_ * 512: b_ * 512 + 256] for b_ in range(4)] + [psB[:, b_ * 512: b_ * 512 + 256] for b_ in range(4)]
        bankk = [A_(b_) for b_ in range(4)] + [B_(b_) for b_ in range(4)]
        for cbk in range(16):
            sA = wq.get(idx[("o", cbk, 0)], ahead=1)
            sB = wq.get(idx[("o", cbk, 1)], ahead=1)
            for hh in range(2):
                xsrc = xs_d[NBLK - 1, hh * 512:(hh + 1) * 512, cbk * 256:(cbk + 1) * 256].rearrange("(t p) c -> p t c", p=128)
                P.dma("sp", lambda e, hh=hh, xsrc=xsrc: e.dma_start(out=xresb[hh], in_=xsrc), writes=xresk[hh], semkey=("xres", hh))

            def mm(e, kk, tt, sA=sA, sB=sB):
                sl = sA if kk < 32 else sB
                return e.matmul(bankap[tt], lhsT=mix(kk)[:, tt * 128:(tt + 1) * 128], rhs=wbuf[:, sl, kk % 32, :],
                                start=(kk == 0), stop=(kk == 63))
            for tt in range(8):
                P.op("pe", lambda e, tt=tt, mm=mm: mm(e, 0, tt), reads=["w%d" % sA] + HK, writes=[bankk[tt]])

            def fa(e, mm=mm):
                ins = None
                for kk in range(1, 32):
                    for tt in range(8):
                        ins = mm(e, kk, tt)
                return ins
            P.op("pe", fa, reads=["w%d" % sA] + HK, writes=bankk)

            def fb(e, mm=mm):
                ins = None
                for kk in range(32, 63):
                    for tt in range(8):
                        ins = mm(e, kk, tt)
                return ins
            P.op("pe", fb, reads=["w%d" % sB] + SCK + YBK, writes=bankk)
            for tt in range(8):
                P.op("pe", lambda e, tt=tt, mm=mm: mm(e, 63, tt), reads=["w%d" % sB] + YBK, writes=[bankk[tt]])
            for hh in range(2):
                xres, yt = xresb[hh], ytb[hh]
                ytk = [("ytb", hh)] + (["rsbc"] if cbk == 0 else [])
                ydst = y_d[hh * 512:(hh + 1) * 512, cbk * 256:(cbk + 1) * 256].rearrange("(t p) c -> p t c", p=128)
                psv = (psA if hh == 0 else psB)[:, :].rearrange("p (b c) -> p b c", b=4)[:, :, 0:256]
                P.op("dve", lambda e, xres=xres, yt=yt, psv=psv: e.tensor_tensor(out=yt, in0=psv, in1=xres, op=ALU.add),
                     reads=bankk[hh * 4:(hh + 1) * 4] + xresk[hh], writes=ytk)
                for tq in range(4):
                    tt = hh * 4 + tq
                    P.op("act", lambda e, tq=tq, tt=tt, yt=yt, cbk=cbk: e.activation(out=junk2, in_=yt[:, tq, :], func=AF.Square,
                                                                                    accum_out=ssqp[:, tt, cbk:cbk + 1]),
                         reads=[("ytb", hh), "ssqp"], writes=[("t2", 0), ("t2", 1), ("ssqp", tt, cbk)])
                P.dma("sp", lambda e, yt=yt, ydst=ydst: e.dma_start(out=ydst, in_=yt),
                      reads=[("ytb", hh)], writes=[("y_d", hh, cbk)], semkey=("yst", hh))

        allq = [("ssqp", a_, b_) for a_ in range(8) for b_ in range(16)]
        P.op("dve", lambda e: e.reduce_sum(out=fst[:, 0, :], in_=ssqp[:], axis=mybir.AxisListType.X), reads=allq + ["ssqp"], writes=["fst0"])
        P.op("act", lambda e: e.activation(out=fst[:, 1, :], in_=fst[:, 0, :], func=AF.Sqrt, scale=1.0 / D, bias=EPS), reads=["fst0"], writes=["fst1"])
        P.op("dve", lambda e: e.reciprocal(out=fst[:, 1, :], in_=fst[:, 1, :]), reads=["fst1"], writes=["fst1"])
        toks = []
        fgb = gbx[:]
        P.dma("sp", lambda e: e.dma_start(out=fgb, in_=fgbc_d[:, :]),
              writes=["gbx", "rsbc", "wsTb", "bspg", ("t2", 0), ("t2", 1), ("ytb", 0), ("ytb", 1)], semkey="fg_ld")
        for tt in range(8):
            pb = tt % 3
            keys = [S_(4 * pb + i) for i in range(4)]
            yt4 = SC[:, 4 * pb: 4 * pb + 4, :].rearrange("p a b -> p (a b)")
            half = tt // 4
            P.dma("sp", lambda e, yt4=yt4, tt=tt: e.dma_start(out=yt4, in_=y_d[tt * 128:(tt + 1) * 128, :]),
                  reads=[("y_d", half, c_) for c_ in range(16)], writes=keys, semkey=("y4", pb))
            P.op("dve", lambda e, yt4=yt4, tt=tt: e.scalar_tensor_tensor(out=yt4, in0=yt4, scalar=fst[:, 1, tt:tt + 1], in1=fgb,
                                                                       op0=ALU.mult, op1=ALU.mult),
                 reads=keys + ["fst1", "gbx"], writes=keys)
            toks.append(P.dma("act", lambda e, yt4=yt4, tt=tt: e.dma_start(out=out_d[tt * 128:(tt + 1) * 128, :], in_=yt4),
                              reads=keys, writes=[("out", tt)], semkey=("ost", pb)))
        P.finish("sp", toks)

        with nc.Block() as block:
            P.emit(block)
    return nc


_NC_CACHE = {}


def kernel(x, norm_g, w_in, conv_w, conv_b, w_gate_a, b_gate_a, w_gate_x, b_gate_x,
           lru_lambda, ln_v_g, ln_v_b, w_spatial, b_spatial, w_out, final_g):
    f = np.float32
    x2 = np.asarray(x, f).reshape(NBLK * TB, D)

    def cm(v):
        return np.ascontiguousarray(np.asarray(v, f).reshape(32, 128).T)
    par = np.stack([cm(conv_b[0]), cm(b_gate_a[0].reshape(-1)), cm(b_gate_x[0].reshape(-1)), cm(lru_lambda[0]),
                    cm(ln_v_g[0]), cm(ln_v_b[0]), cm(norm_g[0])], axis=1).reshape(128, 7 * 32)
    cw = np.ascontiguousarray(np.asarray(conv_w[0], f).reshape(4, 32, 128).transpose(2, 1, 0)).reshape(128, 128)
    shared = {
        "w_in": np.ascontiguousarray(np.asarray(w_in[0], f)),
        "w_out": np.ascontiguousarray(np.asarray(w_out[0], f)),
        "fgbc": np.ascontiguousarray(np.broadcast_to(np.asarray(final_g, f)[None, :], (128, D))),
        "ngbc": np.ascontiguousarray(np.broadcast_to(np.asarray(norm_g[0], f)[None, :], (128, D))),
        "cw": cw,
        "par": np.ascontiguousarray(par),
        "wga": np.ascontiguousarray(np.asarray(w_gate_a[0], f)),
        "wgx": np.ascontiguousarray(np.asarray(w_gate_x[0], f)),
        "wsT": np.ascontiguousarray(np.asarray(w_spatial[0], f).transpose(2, 0, 1)).reshape(128, 16 * 128),
        "bspbc": np.ascontiguousarray(np.broadcast_to(np.asarray(b_spatial[0], f).reshape(1, 16 * 128), (128, 16 * 128))),
    }
    in_maps = []
    for c in range(NCORES):
        xs = np.zeros((NBLK, TB, D), f)
        xs[NBLK - 1 - c:] = x2[: (c + 1) * TB].reshape(c + 1, TB, D)
        cmask = np.zeros((128, NBLK), f)
        cmask[:, NBLK - c:] = 1.0
        m = dict(shared)
        m["xs"] = xs
        m["cmask"] = cmask
        in_maps.append(m)
    if "nc" not in _NC_CACHE:
        _NC_CACHE["nc"] = build_nc()
    res = run_bass_kernel_spmd(_NC_CACHE["nc"], in_maps, core_ids=list(range(NCORES)))
    out = np.concatenate([np.asarray(r["out"], f) for r in res.results], axis=0)
    return out.reshape(1, NBLK * TB, D)
```
